# Optimizing a Trainium2 kernel written in Bass

```python
import math
import jax, jax.numpy as jnp
from jax import lax
import numpy as np

D_MODEL = 2048
BATCH = 8
SEQ = 2048
DEPTH = 2

CHUNK = 64
PLE_DIM = 256
NORM_EPS = 1e-6

SSD_HEADS = 16
SSD_HEAD_DIM = 64
SSD_WIDTH = SSD_HEADS * SSD_HEAD_DIM
SSD_GROUPS = 2
SSD_STATE = 128
SSD_CONV = 4
SSD_CONV_CH = SSD_WIDTH + 2 * SSD_GROUPS * SSD_STATE

GLA_HEADS = 4
GLA_DK = 128
GLA_DV = 256
GLA_WIDTH = GLA_HEADS * GLA_DV
GLA_GATE_RANK = 16
GLA_GATE_NORMALIZER = 16.0

MIX_WIDTH = SSD_WIDTH + GLA_WIDTH

D_FF = 5632
FFN_CONV = 3

IN_SIZES = (SSD_WIDTH, SSD_WIDTH, SSD_GROUPS * SSD_STATE, SSD_GROUPS * SSD_STATE, SSD_HEADS,
            GLA_HEADS * GLA_DK, GLA_HEADS * GLA_DK, GLA_WIDTH, GLA_WIDTH, GLA_GATE_RANK)
IN_COLS = sum(IN_SIZES)

kernel_name = "hymba_ssd_gla_convffn_ple"


def _split_points(sizes):
    pts, acc = [], 0
    for s in sizes[:-1]:
        acc += s
        pts.append(acc)
    return pts


def _rmsnorm(x, g):
    xf = x.astype(jnp.float32)
    y = xf * lax.rsqrt(jnp.mean(xf * xf, axis=-1, keepdims=True) + NORM_EPS)
    return (y * g.astype(jnp.float32)).astype(x.dtype)


def _causal_dwconv(u, w, b):
    k, s = w.shape[0], u.shape[1]
    up = jnp.pad(u, ((0, 0), (k - 1, 0), (0, 0)))
    out = b
    for j in range(k):
        out = out + up[:, j:j + s] * w[j]
    return out


def _chunk_scan(decay, inc):
    def step(h, xs):
        d, u = xs
        return d * h + u, h
    h0 = jnp.zeros_like(inc[:, 0])
    _, h_prev = lax.scan(step, h0, (jnp.moveaxis(decay, 1, 0), jnp.moveaxis(inc, 1, 0)))
    return jnp.moveaxis(h_prev, 0, 1)


def _ssd_mixer(xs, z, bm, cm, dt_raw, conv_w, conv_b, dt_bias, a_log, d_skip, norm_g):
    f32 = jnp.float32
    b, s, _ = xs.shape
    nc = s // CHUNK
    r = SSD_HEADS // SSD_GROUPS
    xbc = jax.nn.silu(_causal_dwconv(jnp.concatenate([xs, bm, cm], axis=-1), conv_w, conv_b))
    xs, bm, cm = jnp.split(xbc, [SSD_WIDTH, SSD_WIDTH + SSD_GROUPS * SSD_STATE], axis=-1)
    x = xs.astype(f32).reshape(b, nc, CHUNK, SSD_GROUPS, r, SSD_HEAD_DIM)
    bc = bm.astype(f32).reshape(b, nc, CHUNK, SSD_GROUPS, SSD_STATE)
    cc = cm.astype(f32).reshape(b, nc, CHUNK, SSD_GROUPS, SSD_STATE)
    dt = jax.nn.softplus(dt_raw.astype(f32) + dt_bias.astype(f32)).reshape(b, nc, CHUNK, SSD_GROUPS, r)
    a = dt * (-jnp.exp(a_log.astype(f32))).reshape(SSD_GROUPS, r)
    a_cs = jnp.cumsum(a, axis=2)
    xdt = x * dt[..., None]
    a_t = jnp.moveaxis(a_cs, 2, -1)
    mask = jnp.tril(jnp.ones((CHUNK, CHUNK), dtype=bool))
    decay = jnp.exp(jnp.where(mask, a_t[..., :, None] - a_t[..., None, :], -jnp.inf))
    cb = jnp.einsum('bclgn,bcsgn->bcgls', cc, bc)
    y_diag = jnp.einsum('bcgls,bcgrls,bcsgrp->bclgrp', cb, decay, xdt)
    states = jnp.einsum('bcsgn,bcsgr,bcsgrp->bcgrpn', bc, jnp.exp(a_cs[:, :, -1:] - a_cs), xdt)
    h_prev = _chunk_scan(jnp.exp(a_cs[:, :, -1])[..., None, None], states)
    y_off = jnp.einsum('bclgn,bcgrpn,bclgr->bclgrp', cc, h_prev, jnp.exp(a_cs))
    y = y_diag + y_off + x * d_skip.astype(f32).reshape(SSD_GROUPS, r, 1)
    y = y.reshape(b, s, SSD_WIDTH) * jax.nn.silu(z.astype(f32))
    y = _rmsnorm(y.reshape(b, s, SSD_GROUPS, SSD_WIDTH // SSD_GROUPS),
                 norm_g.reshape(SSD_GROUPS, SSD_WIDTH // SSD_GROUPS)).reshape(b, s, SSD_WIDTH)
    return y.astype(z.dtype)


def _gla_mixer(q, k, v, g_out, gk_low, gk_up, gk_bias, norm_g):
    f32 = jnp.float32
    b, s, _ = q.shape
    nc = s // CHUNK
    q = q.astype(f32).reshape(b, nc, CHUNK, GLA_HEADS, GLA_DK) * (GLA_DK ** -0.5)
    k = k.astype(f32).reshape(b, nc, CHUNK, GLA_HEADS, GLA_DK)
    v = v.astype(f32).reshape(b, nc, CHUNK, GLA_HEADS, GLA_DV)
    gk = jax.nn.log_sigmoid((gk_low @ gk_up + gk_bias).astype(f32)) / GLA_GATE_NORMALIZER
    bcum = jnp.cumsum(gk.reshape(b, nc, CHUNK, GLA_HEADS, GLA_DK), axis=2)
    q_e = q * jnp.exp(bcum)
    k_e = k * jnp.exp(-bcum)
    mask = jnp.tril(jnp.ones((CHUNK, CHUNK), dtype=bool))
    attn = jnp.where(mask, jnp.einsum('bclhd,bcshd->bchls', q_e, k_e), 0.0)
    o = jnp.einsum('bchls,bcshv->bclhv', attn, v)
    kv = jnp.einsum('bcshd,bcshv->bchdv', k * jnp.exp(bcum[:, :, -1:] - bcum), v)
    s_prev = _chunk_scan(jnp.exp(bcum[:, :, -1])[..., None], kv)
    o = o + jnp.einsum('bclhd,bchdv->bclhv', q_e, s_prev)
    o = _rmsnorm(o, norm_g)
    o = o.reshape(b, s, GLA_WIDTH) * jax.nn.silu(g_out.astype(f32))
    return o.astype(g_out.dtype)


def _normal(k, shape, scale):
    return jax.random.normal(k, shape, jnp.float32) * scale


def setup_inputs(seed: int = 0) -> dict:
    key = jax.random.key(seed)
    ks = jax.random.split(key, 24)
    dt = jnp.exp(jax.random.uniform(ks[6], (DEPTH, SSD_HEADS), jnp.float32)
                 * (math.log(0.1) - math.log(0.001)) + math.log(0.001))
    return {
        "x": _normal(ks[0], (BATCH, SEQ, D_MODEL), 1.0),
        "p": _normal(ks[1], (DEPTH, BATCH, SEQ, PLE_DIM), 1.0),
        "norm_mix": 1.0 + _normal(ks[2], (DEPTH, D_MODEL), 0.02),
        "w_in": _normal(ks[3], (DEPTH, D_MODEL, IN_COLS), D_MODEL ** -0.5),
        "ssd_conv_w": _normal(ks[4], (DEPTH, SSD_CONV, SSD_CONV_CH), SSD_CONV ** -0.5),
        "ssd_conv_b": _normal(ks[5], (DEPTH, SSD_CONV_CH), 0.02),
        "ssd_dt_bias": dt + jnp.log(-jnp.expm1(-dt)),
        "ssd_a_log": jnp.log(jax.random.uniform(ks[7], (DEPTH, SSD_HEADS), jnp.float32, 1.0, 16.0)),
        "ssd_d": 1.0 + _normal(ks[8], (DEPTH, SSD_HEADS), 0.02),
        "ssd_norm": 1.0 + _normal(ks[9], (DEPTH, SSD_WIDTH), 0.02),
        "gla_gk_up": _normal(ks[10], (DEPTH, GLA_GATE_RANK, GLA_HEADS * GLA_DK), GLA_GATE_RANK ** -0.5),
        "gla_gk_bias": _normal(ks[11], (DEPTH, GLA_HEADS * GLA_DK), 0.02),
        "gla_norm": 1.0 + _normal(ks[12], (DEPTH, GLA_DV), 0.02),
        "w_out": _normal(ks[13], (DEPTH, MIX_WIDTH, D_MODEL), MIX_WIDTH ** -0.5),
        "norm_ffn": 1.0 + _normal(ks[14], (DEPTH, D_MODEL), 0.02),
        "ffn_w_up": _normal(ks[15], (DEPTH, D_MODEL, 2 * D_FF), D_MODEL ** -0.5),
        "ffn_conv_w": _normal(ks[16], (DEPTH, FFN_CONV, 2 * D_FF), FFN_CONV ** -0.5),
        "ffn_conv_b": _normal(ks[17], (DEPTH, 2 * D_FF), 0.02),
        "ffn_w_down": _normal(ks[18], (DEPTH, D_FF, D_MODEL), D_FF ** -0.5),
        "norm_ple": 1.0 + _normal(ks[19], (DEPTH, D_MODEL), 0.02),
        "ple_w_gate": _normal(ks[20], (DEPTH, D_MODEL, D_MODEL), D_MODEL ** -0.5),
        "ple_w_proj": _normal(ks[21], (DEPTH, PLE_DIM, D_MODEL), PLE_DIM ** -0.5),
        "norm_final": 1.0 + _normal(ks[22], (D_MODEL,), 0.02),
    }


def reference(x, p, norm_mix, w_in, ssd_conv_w, ssd_conv_b, ssd_dt_bias, ssd_a_log, ssd_d,
              ssd_norm, gla_gk_up, gla_gk_bias, gla_norm, w_out, norm_ffn, ffn_w_up,
              ffn_conv_w, ffn_conv_b, ffn_w_down, norm_ple, ple_w_gate, ple_w_proj, norm_final):
    split_pts = _split_points(IN_SIZES)
    for i in range(DEPTH):
        h = _rmsnorm(x, norm_mix[i])
        proj = h @ w_in[i]
        (s_x, s_z, s_b, s_c, s_dt, g_q, g_k, g_v, g_g, g_gk) = jnp.split(proj, split_pts, axis=-1)
        y_ssd = _ssd_mixer(s_x, s_z, s_b, s_c, s_dt, ssd_conv_w[i], ssd_conv_b[i],
                           ssd_dt_bias[i], ssd_a_log[i], ssd_d[i], ssd_norm[i])
        y_gla = _gla_mixer(g_q, g_k, g_v, g_g, g_gk, gla_gk_up[i], gla_gk_bias[i], gla_norm[i])
        mix = jnp.concatenate([y_ssd, y_gla], axis=-1) @ w_out[i]
        x = x + mix.astype(x.dtype)
        h = _rmsnorm(x, norm_ffn[i])
        u = _causal_dwconv(h @ ffn_w_up[i], ffn_conv_w[i], ffn_conv_b[i])
        gate, val = jnp.split(u, [D_FF], axis=-1)
        x = x + ((jax.nn.silu(gate) * val) @ ffn_w_down[i]).astype(x.dtype)
        pg = jax.nn.sigmoid(_rmsnorm(x, norm_ple[i]) @ ple_w_gate[i])
        x = x + (pg * (p[i] @ ple_w_proj[i])).astype(x.dtype)
    return _rmsnorm(x, norm_final)
```

```python
import contextlib
import math

import numpy as np
import concourse.bass as bass
import concourse.mybir as mybir
from concourse.bass_utils import run_bass_kernel_spmd

F32 = mybir.dt.float32
BF16 = mybir.dt.bfloat16
U8 = mybir.dt.uint8
AF = mybir.ActivationFunctionType
ALU = mybir.AluOpType

D = 2048
KC = 16
T = 512
NSUB = 4
PLE = 256
DFF = 5632
INCOLS = 5664
EPS = 1e-6
ENGS = ("pe", "act", "dve", "pool", "sp")


def _box(ap):
    pat = ap.ap
    esz = mybir.dt.size(ap.dtype)
    pstride = pat[0][0]
    off = int(ap.offset)
    if pstride > 0:
        p0 = off // pstride
        fo = off % pstride
    else:
        p0, fo = 0, off
    ext = 1
    for st, cnt in pat[1:]:
        ext += (cnt - 1) * abs(st)
    return (ap.tensor.name, p0, p0 + pat[0][1], fo * esz, (fo + ext) * esz)


class _Rec:
    __slots__ = ("op", "p0", "p1", "lo", "hi", "w")

    def __init__(self, op, p0, p1, lo, hi, w):
        self.op, self.p0, self.p1, self.lo, self.hi, self.w = op, p0, p1, lo, hi, w


class _Op:
    __slots__ = ("idx", "eng", "fn", "deps", "dma", "ndma", "sig", "need_sig", "waits")

    def __init__(self, idx, eng, fn, dma=None, ndma=0):
        self.idx, self.eng, self.fn, self.dma, self.ndma = idx, eng, fn, dma, ndma
        self.deps = set()
        self.sig = None
        self.need_sig = False
        self.waits = []


class Sched:
    def __init__(self):
        self.ops = []
        self.recs = {}

    def _track(self, op, ap, is_w):
        name, p0, p1, lo, hi = _box(ap)
        lst = self.recs.get(name)
        if lst is None:
            lst = self.recs[name] = []
        keep = []
        for r in lst:
            ov = not (r.hi <= lo or hi <= r.lo or r.p1 <= p0 or p1 <= r.p0)
            if ov and (is_w or r.w) and r.op is not op:
                op.deps.add(r.op)
            cov = ov and r.lo >= lo and r.hi <= hi and r.p0 >= p0 and r.p1 <= p1
            if is_w and cov:
                continue
            if cov and (not is_w) and (not r.w) and r.op.eng == op.eng and r.op.dma is None and op.dma is None:
                continue
            keep.append(r)
        keep.append(_Rec(op, p0, p1, lo, hi, is_w))
        self.recs[name] = keep

    def op(self, eng, fn, reads=(), writes=()):
        o = _Op(len(self.ops), eng, fn)
        self.ops.append(o)
        for a in reads:
            self._track(o, a, False)
        for a in writes:
            self._track(o, a, True)
        return o

    def dma(self, eng, stream, fn, ndma, reads=(), writes=()):
        o = _Op(len(self.ops), eng, fn, dma=stream, ndma=ndma)
        self.ops.append(o)
        for a in reads:
            self._track(o, a, False)
        for a in writes:
            self._track(o, a, True)
        return o

    NSEM = {"w": 2, "ld": 8, "st": 4, "dbg": 4}

    def plan(self):
        streams = sorted({o.dma for o in self.ops if o.dma is not None})
        self.streams = []
        for s_ in streams:
            for i in range(self.NSEM.get(s_, 4)):
                self.streams.append("%s%d" % (s_, i))
        cnt = {e: 0 for e in ENGS}
        scnt = {s: 0 for s in self.streams}
        sidx = {s: 0 for s in streams}
        for o in self.ops:
            if o.dma is not None:
                n = self.NSEM.get(o.dma, 4)
                nm = "%s%d" % (o.dma, sidx[o.dma] % n)
                sidx[o.dma] += 1
                if scnt[nm] > 0:
                    o.waits.append(("s:" + nm, scnt[nm]))
                scnt[nm] += 16 * o.ndma
                o.sig = ("s:" + nm, scnt[nm])
                o.dma = nm
        for o in self.ops:
            for d in o.deps:
                if d.dma is None and not (d.eng == o.eng and d.eng == "pe"):
                    d.need_sig = True
        for o in self.ops:
            if o.dma is not None:
                pass
            elif o.need_sig:
                cnt[o.eng] += 1
                o.sig = ("e:" + o.eng, cnt[o.eng])
        self.final = dict(("s:" + s, v) for s, v in scnt.items())
        seen = {e: {} for e in ENGS}
        for o in self.ops:
            best = {}
            for d in o.deps:
                if d.dma is None and d.eng == o.eng and d.eng == "pe":
                    continue
                k, v = d.sig
                if v > best.get(k, 0):
                    best[k] = v
            sn = seen[o.eng]
            for k, v in best.items():
                if sn.get(k, 0) < v:
                    sn[k] = v
                    o.waits.append((k, v))

    def sem_names(self):
        return ["e:" + e for e in ENGS] + ["s:" + s for s in self.streams]

    def emit(self, block, sems):
        per = {e: [o for o in self.ops if o.eng == e] for e in ENGS}
        final = self.final

        def run(engobj, ename):
            for o in per[ename]:
                for k, v in o.waits:
                    engobj.wait_ge(sems[k], v)
                if o.dma is not None:
                    o.fn(engobj, sems["s:" + o.dma])
                else:
                    ins = o.fn(engobj)
                    if o.need_sig:
                        ins.then_inc(sems[o.sig[0]], 1)
            if ename == "sp":
                for k, v in final.items():
                    if v > 0:
                        engobj.wait_ge(sems[k], v)

        @block.tensor
        def _(e):
            run(e, "pe")

        @block.scalar
        def _(e):
            run(e, "act")

        @block.vector
        def _(e):
            run(e, "dve")

        @block.gpsimd
        def _(e):
            run(e, "pool")

        @block.sync
        def _(e):
            run(e, "sp")


PARAM_NAMES = ["norm_mix", "w_in", "ssd_conv_w", "ssd_conv_b", "ssd_dt_bias", "ssd_a_log", "ssd_d",
               "ssd_norm", "gla_gk_up", "gla_gk_bias", "gla_norm", "w_out", "norm_ffn", "ffn_w_up",
               "ffn_conv_w", "ffn_conv_b", "ffn_w_down", "norm_ple", "ple_w_gate", "ple_w_proj"]


def _consts_np():
    i = np.arange(128)
    ident = np.eye(128, dtype=np.float32)
    tri = (i[:, None] <= i[None, :]).astype(np.float32)
    upper = (i[:, None] > i[None, :]).astype(np.float32)
    maskb = (tri - 1.0) * 30000.0
    ones = np.ones((128, 128), np.float32)
    return np.ascontiguousarray(np.concatenate([ident, tri, upper, maskb, ones], axis=1))


def build_program(S, depth, debug=()):
    assert S % T == 0
    ntile = S // T
    nc = bass.Bass("TRN2", target_bir_lowering=False)
    dr = {}

    def dram_in(name, shape):
        dr[name] = nc.dram_tensor(name, list(shape), F32, kind="ExternalInput").ap()
        return dr[name]

    x_d = dram_in("x", [S, D])
    p_d = dram_in("p", [depth, S, PLE])
    dram_in("norm_mix", [depth, D])
    dram_in("w_in", [depth, D, INCOLS])
    dram_in("ssd_conv_w", [depth, 4, 1536])
    dram_in("ssd_conv_b", [depth, 1536])
    dram_in("ssd_dt_bias", [depth, 16])
    dram_in("ssd_a_log", [depth, 16])
    dram_in("ssd_d", [depth, 16])
    dram_in("ssd_norm", [depth, 1024])
    dram_in("gla_gk_up", [depth, 16, 512])
    dram_in("gla_gk_bias", [depth, 512])
    dram_in("gla_norm", [depth, 256])
    dram_in("w_out", [depth, D, D])
    dram_in("norm_ffn", [depth, D])
    dram_in("ffn_w_up", [depth, D, 2 * DFF])
    dram_in("ffn_conv_w", [depth, 3, 2 * DFF])
    dram_in("ffn_conv_b", [depth, 2 * DFF])
    dram_in("ffn_w_down", [depth, DFF, D])
    dram_in("norm_ple", [depth, D])
    dram_in("ple_w_gate", [depth, D, D])
    dram_in("ple_w_proj", [depth, PLE, D])
    dram_in("norm_final", [D])
    cst_d = dram_in("cst", [128, 640])
    y_d = nc.dram_tensor("y", [S, D], F32, kind="ExternalOutput").ap()
    dbg_out = {}

    S_ = Sched()
    es = contextlib.ExitStack()

    def sb(name, shape, dt):
        return es.enter_context(nc.sbuf_tensor(name, list(shape), dt))

    xT = sb("xT", [128, KC, T], F32)
    hT = sb("hT", [128, KC, T], BF16)
    WSLOT = 16 * 528
    wbuf = [sb(f"wbuf{i}", [128, WSLOT], BF16) for i in range(2)]
    cst = sb("cst_sb", [128, 640], F32)
    identf = cst[:, 0:128]
    tri = cst[:, 128:256]
    upper = cst[:, 256:384]
    maskb = cst[:, 384:512]
    ones = cst[:, 512:640]
    identb = sb("identb", [128, 128], BF16)
    PT_L = 16 * 3 + 48 + 12 + 264 + 88 + 8 + 4
    pt = sb("pt", [128, depth * PT_L + 16], F32)
    O_NMIX, O_NFFN, O_NPLE, O_SCW, O_SCB, O_FCW, O_FCB, O_SN, O_GN = 0, 16, 32, 48, 96, 108, 372, 460, 468
    brow = sb("brow", [128, depth, 3, 16], F32)
    gkup = [sb(f"gkup{l}", [17, 512], F32) for l in range(depth)]
    ssdS = [sb(f"ssdS{l}", [128, 1024], F32) for l in range(depth)]
    ssdSb = [sb(f"ssdSb{l}", [128, 1024], BF16) for l in range(depth)]
    glaS = [sb(f"glaS{l}", [128, 1024], F32) for l in range(depth)]
    glaSb = [sb(f"glaSb{l}", [128, 1024], BF16) for l in range(depth)]
    shalo = [sb(f"shalo{l}", [128, 12, 3], F32) for l in range(depth)]
    fhalo = [sb(f"fhalo{l}", [128, 88, 2], F32) for l in range(depth)]
    RBYTES = 88064
    reg = sb("reg", [128, RBYTES], U8)
    ps = [es.enter_context(nc.psum_tensor(f"ps{i}", [128, 512], F32)) for i in range(8)]

    class Region:
        def __init__(self):
            self.off = 0

        def alloc(self, shape, dt, parts=128):
            n = 1
            for s_ in shape:
                n *= s_
            nbytes = n * mybir.dt.size(dt)
            nbytes_al = (nbytes + 31) // 32 * 32
            assert self.off + nbytes_al <= RBYTES, (self.off, nbytes_al)
            v = reg[0:parts, self.off:self.off + nbytes].bitcast(dt)
            self.off += nbytes_al
            if len(shape) == 2:
                v = v.rearrange("p (a b) -> p a b", a=shape[0])
            elif len(shape) == 3:
                v = v.rearrange("p (a b c) -> p a b c", a=shape[0], b=shape[1])
            return v

    bank_ctr = [0]

    def nb():
        b = bank_ctr[0] % 8
        bank_ctr[0] += 1
        return b

    def ld(dst, src, stream="ld", eng="sp"):
        S_.dma(eng, stream, lambda e, sem: e.dma_start(out=dst, in_=src).then_inc(sem, 16), 1, writes=[dst])

    def st(dst, src, stream="st", eng="sp"):
        S_.dma(eng, stream, lambda e, sem: e.dma_start(out=dst, in_=src).then_inc(sem, 16), 1, reads=[src])

    def dbg(name, ap, dt=F32):
        if name not in debug:
            return
        shp = list(ap.shape)
        o = nc.dram_tensor("dbg_" + name, shp, ap.dtype, kind="ExternalOutput").ap()
        dbg_out[name] = o
        st(o, ap, stream="dbg")

    def act_(fn, reads, writes):
        return S_.op("act", fn, reads, writes)

    def dve_(fn, reads, writes):
        return S_.op("dve", fn, reads, writes)

    def pool_(fn, reads, writes):
        return S_.op("pool", fn, reads, writes)

    def pe_(fn, reads, writes):
        return S_.op("pe", fn, reads, writes)

    def bc3(ap2, inner):
        sh = list(ap2.shape)
        return ap2.unsqueeze(2).to_broadcast([sh[0], sh[1], inner])

    ld(cst[:], cst_d)
    dve_(lambda e: e.tensor_copy(out=identb[:], in_=identf), [identf], [identb[:]])
    for l in range(depth):
        for a_ in (ssdS[l], glaS[l]):
            pool_(lambda e, a_=a_: e.memset(a_[:], 0.0), [], [a_[:]])
        for a_ in (ssdSb[l], glaSb[l]):
            pool_(lambda e, a_=a_: e.memset(a_[:], 0.0), [], [a_[:]])
        pool_(lambda e, l=l: e.memset(shalo[l][:], 0.0), [], [shalo[l][:]])
        pool_(lambda e, l=l: e.memset(fhalo[l][:], 0.0), [], [fhalo[l][:]])

    R = Region()
    pstage = R.alloc([128], F32)

    def load_cols(src2d, nrows, col0):
        ld(pstage[0:nrows, :], src2d)
        b = nb()
        pe_(lambda e: e.transpose(ps[b][:, 0:nrows], pstage[0:nrows, :], identf[0:nrows, 0:nrows]),
            [pstage[0:nrows, :], identf], [ps[b][:, 0:nrows]])
        act_(lambda e: e.copy(out=pt[:, col0:col0 + nrows], in_=ps[b][:, 0:nrows]),
             [ps[b][:, 0:nrows]], [pt[:, col0:col0 + nrows]])

    for l in range(depth):
        base = l * PT_L
        load_cols(dr["norm_mix"][l].rearrange("(k p) -> k p", p=128), 16, base + O_NMIX)
        load_cols(dr["norm_ffn"][l].rearrange("(k p) -> k p", p=128), 16, base + O_NFFN)
        load_cols(dr["norm_ple"][l].rearrange("(k p) -> k p", p=128), 16, base + O_NPLE)
        load_cols(dr["ssd_conv_w"][l].rearrange("j (t p) -> (j t) p", p=128), 48, base + O_SCW)
        load_cols(dr["ssd_conv_b"][l].rearrange("(t p) -> t p", p=128), 12, base + O_SCB)
        for j in range(3):
            load_cols(dr["ffn_conv_w"][l, j].rearrange("(t p) -> t p", p=128), 88, base + O_FCW + 88 * j)
        load_cols(dr["ffn_conv_b"][l].rearrange("(t p) -> t p", p=128), 88, base + O_FCB)
        load_cols(dr["ssd_norm"][l].rearrange("(t p) -> t p", p=128), 8, base + O_SN)
        load_cols(dr["gla_norm"][l].rearrange("(t p) -> t p", p=128), 2, base + O_GN)
        load_cols(dr["gla_norm"][l].rearrange("(t p) -> t p", p=128), 2, base + O_GN + 2)
        ld(brow[:, l, 0, :], dr["ssd_dt_bias"][l].partition_broadcast(128))
        ld(brow[:, l, 1, :], dr["ssd_a_log"][l].partition_broadcast(128))
        ld(brow[:, l, 2, :], dr["ssd_d"][l].partition_broadcast(128))
        act_(lambda e, l=l: e.activation(out=brow[:, l, 1, :], in_=brow[:, l, 1, :], func=AF.Exp),
             [brow[:, l, 1, :]], [brow[:, l, 1, :]])
        dve_(lambda e, l=l: e.tensor_scalar(out=brow[:, l, 1, :], in0=brow[:, l, 1, :], scalar1=-1.0, scalar2=None,
                                            op0=ALU.mult), [brow[:, l, 1, :]], [brow[:, l, 1, :]])
        ld(gkup[l][0:16, :], dr["gla_gk_up"][l])
        ld(gkup[l][16:17, :], dr["gla_gk_bias"][l].rearrange("(o n) -> o n", o=1))
    O_NFIN = depth * PT_L
    load_cols(dr["norm_final"].rearrange("(k p) -> k p", p=128), 16, O_NFIN)

    jobs = []

    def add_job(src2d, nk, ncols, consumer):
        jobs.append((src2d, nk, ncols, consumer))

    def issue_load(ji):
        src2d, nk, ncols, _ = jobs[ji]
        slot = wbuf[ji % 2]
        view = slot[:, 0:nk * ncols].rearrange("p (k n) -> p k n", k=nk)
        srcv = src2d.rearrange("(k p) n -> p k n", p=128)
        groups = [(k0, min(4, nk - k0)) for k0 in range(0, nk, 4)]

        def fn(e, sem):
            for k0, kk in groups:
                e.dma_start(out=view[:, k0:k0 + kk, :], in_=srcv[:, k0:k0 + kk, :]).then_inc(sem, 16)

        S_.dma("pool", "w", fn, len(groups), writes=[view])
        return view

    def rmsnorm_to_hT(gcol0, out_tile=None, out_f32=None):
        mark = R.off
        sq = [R.alloc([T], F32) for _ in range(2)]
        rs = R.alloc([T], F32)
        b = nb()
        for k in range(KC):
            s_ = sq[k % 2]
            act_(lambda e, s_=s_, k=k: e.activation(out=s_, in_=xT[:, k, :], func=AF.Square),
                 [xT[:, k, :]], [s_])
            pe_(lambda e, s_=s_, k=k: e.matmul(ps[b][:, :], lhsT=ones, rhs=s_, start=(k == 0), stop=(k == KC - 1)),
                [ones, s_], [ps[b][:, :]])

        act_(lambda e: e.activation(out=rs, in_=ps[b][:, :], func=AF.Ln, scale=1.0 / D, bias=EPS), [ps[b][:, :]], [rs])
        act_(lambda e: e.activation(out=rs, in_=rs, func=AF.Exp, scale=-0.5), [rs], [rs])
        for k in range(KC):
            dst = hT[:, k, :] if out_f32 is None else out_f32[:, k, :]
            dve_(lambda e, k=k, dst=dst: e.scalar_tensor_tensor(out=dst, in0=xT[:, k, :],
                                                                scalar=pt[:, gcol0 + k:gcol0 + k + 1], in1=rs,
                                                                op0=ALU.mult, op1=ALU.mult),
                 [xT[:, k, :], rs, pt[:, gcol0 + k:gcol0 + k + 1]], [dst])
        R.off = mark

    def fm_matmul(view, c0, M, rhsT, bank, nk=KC):
        def fn(e):
            ins = None
            for k in range(nk):
                ins = e.matmul(ps[bank][0:M, :], lhsT=view[:, k, c0:c0 + M], rhs=rhsT[:, k, :],
                               start=(k == 0), stop=(k == nk - 1))
            return ins
        pe_(fn, [view[:, :, c0:c0 + M], rhsT[:, 0:nk, :]], [ps[bank][0:M, :]])

    def tm_matmul(view, c0, n, sub, bank):
        def fn(e):
            ins = None
            for k in range(KC):
                ins = e.matmul(ps[bank][:, 0:n], lhsT=hT[:, k, sub * 128:(sub + 1) * 128], rhs=view[:, k, c0:c0 + n],
                               start=(k == 0), stop=(k == KC - 1))
            return ins
        pe_(fn, [view[:, :, c0:c0 + n], hT[:, :, sub * 128:(sub + 1) * 128]], [ps[bank][:, 0:n]])

    def layer_tile(l, ti):
        base = l * PT_L
        t0 = ti * T
        st_ = {}

        def begin_mixer():
            R.off = 0
            rmsnorm_to_hT(base + O_NMIX)
            st_["xbcT"] = R.alloc([12, T], BF16)
            st_["qkT"] = R.alloc([8, T], BF16)
            st_["gkT"] = R.alloc([T], F32)
            st_["ztm"] = R.alloc([NSUB, 1024], BF16)
            st_["vtm"] = R.alloc([NSUB, 1024], BF16)
            st_["gtm"] = R.alloc([NSUB, 1024], BF16)
            st_["ktm"] = R.alloc([NSUB, 512], BF16)
            st_["dtraw"] = R.alloc([NSUB, 16], F32)
            st_["mark"] = R.off
            st_["cin"] = [R.alloc([516], F32) for _ in range(2)]
            st_["acc"] = [R.alloc([T], F32) for _ in range(2)]
            st_["ci"] = 0
            tap("hT0", hT[:])
            gkT = st_["gkT"]
            pool_(lambda e: e.memset(gkT[0:17, :], 1.0), [], [gkT[0:17, :]])

        def conv_evac(bank, ct):
            i = st_["ci"] % 2
            st_["ci"] += 1
            cin, acc = st_["cin"][i], st_["acc"][i]
            xbcT = st_["xbcT"]
            act_(lambda e: e.copy(out=cin[:, 3:515], in_=ps[bank][:, :]), [ps[bank][:, :]], [cin[:, 3:515]])
            wc = base + O_SCW
            bcol = base + O_SCB + ct

            dve_(lambda e: e.tensor_copy(out=cin[:, 0:3], in_=shalo[l][:, ct, :]), [shalo[l][:, ct, :]], [cin[:, 0:3]])
            dve_(lambda e: e.tensor_scalar(out=acc, in0=cin[:, 0:512], scalar1=pt[:, wc + ct:wc + ct + 1],
                                           scalar2=pt[:, bcol:bcol + 1], op0=ALU.mult, op1=ALU.add),
                 [cin[:, 0:512]], [acc])
            for j in range(1, 4):
                dve_(lambda e, j=j: e.scalar_tensor_tensor(out=acc, in0=cin[:, j:j + 512],
                                                           scalar=pt[:, wc + 12 * j + ct:wc + 12 * j + ct + 1], in1=acc,
                                                           op0=ALU.mult, op1=ALU.add),
                     [cin[:, j:j + 512], acc], [acc])
            dve_(lambda e: e.tensor_copy(out=shalo[l][:, ct, :], in_=cin[:, 512:515]), [cin[:, 512:515]], [shalo[l][:, ct, :]])
            act_(lambda e: e.activation(out=xbcT[:, ct, :], in_=acc, func=AF.Silu), [acc], [xbcT[:, ct, :]])

        def job_fm_conv(ct0):
            def consumer(view):
                for j in range(4):
                    b = nb()
                    fm_matmul(view, j * 128, 128, hT, b)
                    conv_evac(b, ct0 + j)
            return consumer

        def job_tm(dst_name, c_dst, toact):
            def consumer(view):
                for sub in range(NSUB):
                    b = nb()
                    tm_matmul(view, 0, 512, sub, b)
                    dst = st_[dst_name][:, sub, c_dst:c_dst + 512]
                    if toact:
                        act_(lambda e, dst=dst, b=b: e.copy(out=dst, in_=ps[b][:, :]), [ps[b][:, :]], [dst])
                    else:
                        dve_(lambda e, dst=dst, b=b: e.tensor_copy(out=dst, in_=ps[b][:, :]), [ps[b][:, :]], [dst])
            return consumer

        def job_dt_q(view):
            for sub in range(NSUB):
                b = nb()
                tm_matmul(view, 0, 16, sub, b)
                dst = st_["dtraw"][:, sub, :]
                dve_(lambda e, dst=dst, b=b: e.tensor_copy(out=dst, in_=ps[b][:, 0:16]), [ps[b][:, 0:16]], [dst])
            for j in range(4):
                b = nb()
                fm_matmul(view, 16 + j * 128, 128, hT, b)
                dst = st_["qkT"][:, j, :]
                act_(lambda e, dst=dst, b=b: e.activation(out=dst, in_=ps[b][:, :], func=AF.Copy, scale=128.0 ** -0.5),
                     [ps[b][:, :]], [dst])

        def job_k(view):
            for j in range(4):
                b = nb()
                fm_matmul(view, j * 128, 128, hT, b)
                dst = st_["qkT"][:, 4 + j, :]
                act_(lambda e, dst=dst, b=b: e.copy(out=dst, in_=ps[b][:, :]), [ps[b][:, :]], [dst])
            for sub in range(NSUB):
                b = nb()
                tm_matmul(view, 0, 512, sub, b)
                dst = st_["ktm"][:, sub, :]
                dve_(lambda e, dst=dst, b=b: e.tensor_copy(out=dst, in_=ps[b][:, :]), [ps[b][:, :]], [dst])

        def job_g1_gk(view):
            job_tm("gtm", 512, True)(view)
            b = nb()
            fm_matmul(view, 512, 16, hT, b)
            gkT = st_["gkT"]
            act_(lambda e: e.copy(out=gkT[0:16, :], in_=ps[b][0:16, :]), [ps[b][0:16, :]], [gkT[0:16, :]])
            R.off = st_["mark"]
            for sub in range(NSUB):
                mixer_sub(sub)

        def mixer_sub(sub):
            mark = R.off
            c0 = sub * 128
            xbcT, qkT, gkT = st_["xbcT"], st_["qkT"], st_["gkT"]
            ztm, vtm, gtm, ktm, dtraw = st_["ztm"], st_["vtm"], st_["gtm"], st_["ktm"], st_["dtraw"]
            xtm = R.alloc([16, 64], BF16)
            btm = R.alloc([256], BF16)
            sm = R.alloc([12, 16], F32)
            acsT = R.alloc([128], F32)
            nacsT = R.alloc([128], F32)
            xdt = R.alloc([16, 64], BF16)
            xdtw = R.alloc([16, 64], BF16)
            cbT = R.alloc([2, 128], F32)
            dec = [R.alloc([4, 128], F32) for _ in range(2)]
            MT = R.alloc([16, 128], BF16)
            ya = R.alloc([2, 512], F32)
            sz = R.alloc([2, 512], F32)
            ss2 = R.alloc([2], F32)
            xtm2 = xtm.rearrange("p h d -> p (h d)")
            b1 = nb()
            pbv = ps[b1][:, :].bitcast(BF16)
            def tr_x(e):
                ins = None
                for c in range(8):
                    ins = e.transpose(pbv[:, c * 128:(c + 1) * 128], xbcT[:, c, c0:c0 + 128], identb[:])
                return ins
            pe_(tr_x, [xbcT[:, 0:8, c0:c0 + 128], identb[:]], [ps[b1][:, :]])
            act_(lambda e: e.copy(out=xtm2, in_=pbv), [ps[b1][:, :]], [xtm2])
            b2 = nb()
            pbv2 = ps[b2][:, 0:128].bitcast(BF16)
            def tr_b(e):
                ins = None
                for c in range(2):
                    ins = e.transpose(pbv2[:, c * 128:(c + 1) * 128], xbcT[:, 8 + c, c0:c0 + 128], identb[:])
                return ins
            pe_(tr_b, [xbcT[:, 8:10, c0:c0 + 128], identb[:]], [ps[b2][:, 0:128]])
            dve_(lambda e: e.tensor_copy(out=btm, in_=pbv2), [ps[b2][:, 0:128]], [btm])
            dt_, a_, acs, eacs, etot, warg, wst, dw, t16, tab, tmx = [sm[:, i, :] for i in range(11)]
            dtb, Ab, Db = brow[:, l, 0, :], brow[:, l, 1, :], brow[:, l, 2, :]
            dve_(lambda e: e.tensor_tensor(out=t16, in0=dtraw[:, sub, :], in1=dtb, op=ALU.add), [dtraw[:, sub, :], dtb], [t16])
            dve_(lambda e: e.tensor_scalar(out=tab, in0=t16, scalar1=-1.0, scalar2=None, op0=ALU.mult), [t16], [tab])
            dve_(lambda e: e.tensor_tensor(out=tab, in0=tab, in1=t16, op=ALU.min), [tab, t16], [tab])
            dve_(lambda e: e.tensor_scalar(out=tmx, in0=t16, scalar1=0.0, scalar2=None, op0=ALU.max), [t16], [tmx])
            act_(lambda e: e.activation(out=tab, in_=tab, func=AF.Exp), [tab], [tab])
            act_(lambda e: e.activation(out=tab, in_=tab, func=AF.Ln, bias=1.0), [tab], [tab])
            dve_(lambda e: e.tensor_tensor(out=dt_, in0=tmx, in1=tab, op=ALU.add), [tmx, tab], [dt_])
            dve_(lambda e: e.tensor_tensor(out=a_, in0=dt_, in1=Ab, op=ALU.mult), [dt_, Ab], [a_])
            b3 = nb()
            def f_cs(e):
                e.matmul(ps[b3][:, 0:16], lhsT=tri, rhs=a_, start=True, stop=True)
                e.matmul(ps[b3][:, 16:32], lhsT=ones, rhs=a_, start=True, stop=True)
                return e.matmul(ps[b3][0:16, 128:256], lhsT=a_, rhs=tri, start=True, stop=True)
            pe_(f_cs, [tri, ones, a_], [ps[b3][:, 0:32], ps[b3][0:16, 128:256]])
            def f_cs2(e):
                e.activation(out=eacs, in_=ps[b3][:, 0:16], func=AF.Exp)
                e.activation(out=etot, in_=ps[b3][:, 16:32], func=AF.Exp)
                e.copy(out=acs, in_=ps[b3][:, 0:16])
                return e.copy(out=acsT[0:16, :], in_=ps[b3][0:16, 128:256])
            act_(f_cs2, [ps[b3][:, 0:32], ps[b3][0:16, 128:256]], [eacs, etot, acs, acsT[0:16, :]])
            def f_cs3(e):
                e.tensor_scalar(out=nacsT[0:16, :], in0=ps[b3][0:16, 128:256], scalar1=-1.0, scalar2=None, op0=ALU.mult)
                return e.tensor_tensor(out=warg, in0=ps[b3][:, 16:32], in1=acs, op=ALU.subtract)
            dve_(f_cs3, [ps[b3][0:16, 128:256], ps[b3][:, 16:32], acs], [nacsT[0:16, :], warg])
            act_(lambda e: e.activation(out=wst, in_=warg, func=AF.Exp), [warg], [wst])
            dve_(lambda e: e.tensor_tensor(out=dw, in0=dt_, in1=wst, op=ALU.mult), [dt_, wst], [dw])
            dve_(lambda e: e.tensor_tensor(out=xdt, in0=xtm, in1=bc3(dt_, 64), op=ALU.mult), [xtm, dt_], [xdt])
            dve_(lambda e: e.tensor_tensor(out=xdtw, in0=xtm, in1=bc3(dw, 64), op=ALU.mult), [xtm, dw], [xdtw])
            b4 = nb()
            def f_cb(e):
                ins = None
                for g in range(2):
                    ins = e.matmul(ps[b4][:, g * 128:(g + 1) * 128], lhsT=xbcT[:, 8 + g, c0:c0 + 128],
                                   rhs=xbcT[:, 10 + g, c0:c0 + 128], start=True, stop=True)
                return ins
            pe_(f_cb, [xbcT[:, 8:12, c0:c0 + 128]], [ps[b4][:, 0:256]])
            act_(lambda e: e.copy(out=cbT.rearrange("p g n -> p (g n)"), in_=ps[b4][:, 0:256]),
                 [ps[b4][:, 0:256]], [cbT])
            for hb in range(4):
                b5 = nb()
                def f_dec(e, hb=hb, b5=b5):
                    ins = None
                    for hh in range(4):
                        h = hb * 4 + hh
                        Eh = identf[0:16, h:h + 1].to_broadcast([16, 128])
                        o_ = ps[b5][:, hh * 128:(hh + 1) * 128]
                        e.matmul(o_, lhsT=Eh, rhs=acsT[0:16, :], start=True, stop=False)
                        e.matmul(o_, lhsT=nacsT[0:16, :], rhs=Eh, start=False, stop=False)
                        ins = e.matmul(o_, lhsT=identf, rhs=maskb, start=False, stop=True)
                    return ins
                pe_(f_dec, [identf, maskb, acsT[0:16, :], nacsT[0:16, :]], [ps[b5][:, :]])
                d_ = dec[hb % 2]
                act_(lambda e, d_=d_, b5=b5: e.activation(out=d_.rearrange("p h n -> p (h n)"), in_=ps[b5][:, :], func=AF.Exp),
                     [ps[b5][:, :]], [d_])
                g = hb // 2
                dve_(lambda e, d_=d_, hb=hb, g=g: e.tensor_tensor(out=MT[:, hb * 4:(hb + 1) * 4, :], in0=d_,
                                                                  in1=cbT[:, g:g + 1, :].to_broadcast([128, 4, 128]),
                                                                  op=ALU.mult),
                     [d_, cbT[:, g, :]], [MT[:, hb * 4:(hb + 1) * 4, :]])
            by = [nb(), nb()]
            def f_yd(e):
                ins = None
                for h in range(16):
                    ins = e.matmul(ps[by[h // 8]][:, (h % 8) * 64:(h % 8) * 64 + 64], lhsT=MT[:, h, :], rhs=xdt[:, h, :],
                                   start=True, stop=True)
                return ins
            pe_(f_yd, [MT, xdt], [ps[by[0]][:, :], ps[by[1]][:, :]])
            bo = [nb(), nb()]
            def f_yo(e):
                ins = None
                for g in range(2):
                    ins = e.matmul(ps[bo[g]][:, :], lhsT=xbcT[:, 10 + g, c0:c0 + 128],
                                   rhs=ssdSb[l][:, g * 512:(g + 1) * 512], start=True, stop=True)
                return ins
            pe_(f_yo, [xbcT[:, 10:12, c0:c0 + 128], ssdSb[l][:, :]], [ps[bo[0]][:, :], ps[bo[1]][:, :]])
            for g in range(2):
                yag = ya[:, g, :].rearrange("p (h d) -> p h d", h=8)
                dve_(lambda e, g=g, yag=yag: e.tensor_tensor(out=yag, in0=ps[bo[g]][:, :].rearrange("p (h d) -> p h d", h=8),
                                                             in1=bc3(eacs[:, g * 8:(g + 1) * 8], 64), op=ALU.mult),
                     [ps[bo[g]][:, :], eacs], [ya[:, g, :]])
                dve_(lambda e, g=g: e.tensor_tensor(out=ya[:, g, :], in0=ya[:, g, :], in1=ps[by[g]][:, :], op=ALU.add),
                     [ya[:, g, :], ps[by[g]][:, :]], [ya[:, g, :]])
                szg = sz[:, g, :].rearrange("p (h d) -> p h d", h=8)
                dve_(lambda e, g=g, szg=szg: e.tensor_tensor(out=szg, in0=xtm[:, g * 8:(g + 1) * 8, :],
                                                             in1=bc3(Db[:, g * 8:(g + 1) * 8], 64), op=ALU.mult),
                     [xtm[:, g * 8:(g + 1) * 8, :], Db], [sz[:, g, :]])
                dve_(lambda e, g=g: e.tensor_tensor(out=ya[:, g, :], in0=ya[:, g, :], in1=sz[:, g, :], op=ALU.add),
                     [ya[:, g, :], sz[:, g, :]], [ya[:, g, :]])
            bs = [nb(), nb()]
            def f_st(e):
                ins = None
                for g in range(2):
                    ins = e.matmul(ps[bs[g]][:, :], lhsT=btm[:, g * 128:(g + 1) * 128],
                                   rhs=xdtw[:, g * 8:(g + 1) * 8, :].rearrange("p h d -> p (h d)"), start=True, stop=True)
                return ins
            pe_(f_st, [btm, xdtw], [ps[bs[0]][:, :], ps[bs[1]][:, :]])
            for g in range(2):
                Sg = ssdS[l][:, g * 512:(g + 1) * 512]
                dve_(lambda e, g=g, Sg=Sg: e.tensor_tensor(out=Sg.rearrange("p (h d) -> p h d", h=8),
                                                           in0=Sg.rearrange("p (h d) -> p h d", h=8),
                                                           in1=bc3(etot[:, g * 8:(g + 1) * 8], 64), op=ALU.mult),
                     [Sg, etot], [Sg])
                dve_(lambda e, g=g, Sg=Sg: e.tensor_tensor(out=Sg, in0=Sg, in1=ps[bs[g]][:, :], op=ALU.add),
                     [Sg, ps[bs[g]][:, :]], [Sg])
                act_(lambda e, g=g, Sg=Sg: e.copy(out=ssdSb[l][:, g * 512:(g + 1) * 512], in_=Sg), [Sg],
                     [ssdSb[l][:, g * 512:(g + 1) * 512]])
            if l == 0 and ti == 0 and sub == 0:
                dbg("s_dt", dt_); dbg("s_a", a_); dbg("s_acs", acs); dbg("s_acsT", acsT[0:16, :]); dbg("s_nacsT", nacsT[0:16, :])
                dbg("s_cbT", cbT); dbg("s_MT", MT); dbg("s_xtm", xtm); dbg("s_xdt", xdt); dbg("s_yraw", ya); dbg("s_btm", btm)
            szf = sz.rearrange("p g n -> p (g n)")
            yaf = ya.rearrange("p g n -> p (g n)")
            act_(lambda e: e.activation(out=szf, in_=ztm[:, sub, :], func=AF.Silu), [ztm[:, sub, :]], [sz])
            dve_(lambda e: e.tensor_tensor(out=yaf, in0=yaf, in1=szf, op=ALU.mult), [ya, sz], [ya])
            dve_(lambda e: e.memset(ss2, 0.0), [], [ss2])
            def f_ss(e):
                ins = None
                for g in range(2):
                    ins = e.activation(out=sz[:, g, :], in_=ya[:, g, :], func=AF.Square, accum_out=ss2[:, g:g + 1])
                return ins
            act_(f_ss, [ya, ss2], [sz, ss2])
            act_(lambda e: e.activation(out=ss2, in_=ss2, func=AF.Ln, scale=1.0 / 512, bias=EPS), [ss2], [ss2])
            act_(lambda e: e.activation(out=ss2, in_=ss2, func=AF.Exp, scale=-0.5), [ss2], [ss2])
            def f_yn(e):
                ins = None
                for g in range(2):
                    ins = e.tensor_scalar(out=ya[:, g, :], in0=ya[:, g, :], scalar1=ss2[:, g:g + 1], scalar2=None, op0=ALU.mult)
                return ins
            dve_(f_yn, [ya, ss2], [ya])
            if l == 0 and ti == 0 and sub == 0:
                dbg("ya", yaf)
            for half in range(2):
                b = nb()
                def f_tr(e, half=half, b=b):
                    ins = None
                    for j in range(4):
                        c = half * 4 + j
                        ins = e.transpose(ps[b][:, j * 128:(j + 1) * 128], yaf[:, c * 128:(c + 1) * 128], identf)
                    return ins
                pe_(f_tr, [yaf[:, half * 512:(half + 1) * 512], identf], [ps[b][:, :]])
                gc = base + O_SN + half * 4
                dve_(lambda e, half=half, b=b, gc=gc: e.tensor_tensor(
                    out=hT[:, half * 4:half * 4 + 4, c0:c0 + 128], in0=ps[b][:, :].rearrange("p (j n) -> p j n", j=4),
                    in1=bc3(pt[:, gc:gc + 4], 128), op=ALU.mult),
                    [ps[b][:, :], pt[:, gc:gc + 4]], [hT[:, half * 4:half * 4 + 4, c0:c0 + 128]])
            R.off = mark
            Lb = R.alloc([512], F32)
            tA = R.alloc([512], F32)
            tB = R.alloc([512], F32)
            Eq = R.alloc([4, 128], F32)
            Ek = R.alloc([4, 128], F32)
            Er = R.alloc([512], F32)
            qe = R.alloc([4, 128], BF16)
            ke = R.alloc([4, 128], BF16)
            kw = R.alloc([512], BF16)
            am = R.alloc([4, 128], BF16)
            G = R.alloc([1024], F32)
            yg = R.alloc([1024], F32)
            ssg = R.alloc([4], F32)
            bg = nb()
            pe_(lambda e: e.matmul(ps[bg][:, :], lhsT=gkT[0:17, c0:c0 + 128], rhs=gkup[l][0:17, :], start=True, stop=True),
                [gkT[0:17, c0:c0 + 128], gkup[l][0:17, :]], [ps[bg][:, :]])
            dve_(lambda e: e.tensor_scalar(out=tB, in0=ps[bg][:, :], scalar1=-1.0, scalar2=None, op0=ALU.mult),
                 [ps[bg][:, :]], [tB])
            dve_(lambda e: e.tensor_tensor(out=tA, in0=tB, in1=ps[bg][:, :], op=ALU.min), [tB, ps[bg][:, :]], [tA])
            dve_(lambda e: e.tensor_scalar(out=tB, in0=tB, scalar1=0.0, scalar2=None, op0=ALU.max), [tB], [tB])
            act_(lambda e: e.activation(out=tA, in_=tA, func=AF.Exp), [tA], [tA])
            act_(lambda e: e.activation(out=tA, in_=tA, func=AF.Ln, bias=1.0), [tA], [tA])
            dve_(lambda e: e.tensor_tensor(out=Lb, in0=tA, in1=tB, op=ALU.add), [tA, tB], [Lb])
            bcm = nb()
            def f_cum(e):
                ins = None
                for h in range(4):
                    ins = e.matmul(ps[bcm][:, h * 128:(h + 1) * 128], lhsT=Lb[:, h * 128:(h + 1) * 128], rhs=tri,
                                   start=True, stop=True)
                return ins
            pe_(f_cum, [Lb, tri], [ps[bcm][:, :]])
            br = nb()
            pe_(lambda e: e.matmul(ps[br][:, :], lhsT=upper, rhs=Lb, start=True, stop=True), [upper, Lb], [ps[br][:, :]])
            def f_ex(e):
                e.activation(out=Eq.rearrange("p h n -> p (h n)"), in_=ps[bcm][:, :], func=AF.Exp, scale=-1.0 / 16)
                e.activation(out=Ek.rearrange("p h n -> p (h n)"), in_=ps[bcm][:, :], func=AF.Exp, scale=1.0 / 16)
                return e.activation(out=Er, in_=ps[br][:, :], func=AF.Exp, scale=-1.0 / 16)
            act_(f_ex, [ps[bcm][:, :], ps[br][:, :]], [Eq, Ek, Er])
            def f_qk(e):
                e.tensor_tensor(out=qe, in0=qkT[:, 0:4, c0:c0 + 128], in1=Eq, op=ALU.mult)
                e.tensor_tensor(out=ke, in0=qkT[:, 4:8, c0:c0 + 128], in1=Ek, op=ALU.mult)
                return e.tensor_tensor(out=kw, in0=ktm[:, sub, :], in1=Er, op=ALU.mult)
            dve_(f_qk, [qkT[:, :, c0:c0 + 128], Eq, Ek, ktm[:, sub, :], Er], [qe, ke, kw])
            ba = nb()
            def f_at(e):
                ins = None
                for h in range(4):
                    ins = e.matmul(ps[ba][:, h * 128:(h + 1) * 128], lhsT=ke[:, h, :], rhs=qe[:, h, :], start=True, stop=True)
                return ins
            pe_(f_at, [ke, qe], [ps[ba][:, :]])
            dve_(lambda e: e.tensor_tensor(out=am, in0=ps[ba][:, :].rearrange("p (h n) -> p h n", h=4),
                                           in1=tri.unsqueeze(1).to_broadcast([128, 4, 128]), op=ALU.mult),
                 [ps[ba][:, :], tri], [am])
            bo2 = [nb(), nb()]
            def f_o(e):
                ins = None
                for h in range(4):
                    o_ = ps[bo2[h // 2]][:, (h % 2) * 256:(h % 2) * 256 + 256]
                    e.matmul(o_, lhsT=am[:, h, :], rhs=vtm[:, sub, h * 256:(h + 1) * 256], start=True, stop=False)
                    ins = e.matmul(o_, lhsT=qe[:, h, :], rhs=glaSb[l][:, h * 256:(h + 1) * 256], start=False, stop=True)
                return ins
            pe_(f_o, [am, qe, vtm[:, sub, :], glaSb[l][:, :]], [ps[bo2[0]][:, :], ps[bo2[1]][:, :]])
            bk = [nb(), nb()]
            def f_kv(e):
                ins = None
                for h in range(4):
                    ins = e.matmul(ps[bk[h // 2]][:, (h % 2) * 256:(h % 2) * 256 + 256], lhsT=kw[:, h * 128:(h + 1) * 128],
                                   rhs=vtm[:, sub, h * 256:(h + 1) * 256], start=True, stop=True)
                return ins
            pe_(f_kv, [kw, vtm[:, sub, :]], [ps[bk[0]][:, :], ps[bk[1]][:, :]])
            def f_gs(e):
                ins = None
                for h in range(4):
                    Sh = glaS[l][:, h * 256:(h + 1) * 256]
                    ins = e.scalar_tensor_tensor(out=Sh, in0=Sh, scalar=Eq[:, h, 127:128],
                                                 in1=ps[bk[h // 2]][:, (h % 2) * 256:(h % 2) * 256 + 256],
                                                 op0=ALU.mult, op1=ALU.add)
                return ins
            dve_(f_gs, [glaS[l][:, :], Eq, ps[bk[0]][:, :], ps[bk[1]][:, :]], [glaS[l][:, :]])
            act_(lambda e: e.copy(out=glaSb[l][:, :], in_=glaS[l][:, :]), [glaS[l][:, :]], [glaSb[l][:, :]])
            dve_(lambda e: e.memset(ssg, 0.0), [], [ssg])
            def f_gn(e):
                ins = None
                for h in range(4):
                    ins = e.activation(out=yg[:, h * 256:(h + 1) * 256], in_=ps[bo2[h // 2]][:, (h % 2) * 256:(h % 2) * 256 + 256],
                                       func=AF.Square, accum_out=ssg[:, h:h + 1])
                return ins
            act_(f_gn, [ps[bo2[0]][:, :], ps[bo2[1]][:, :], ssg], [yg, ssg])
            act_(lambda e: e.activation(out=ssg, in_=ssg, func=AF.Ln, scale=1.0 / 256, bias=EPS), [ssg], [ssg])
            act_(lambda e: e.activation(out=ssg, in_=ssg, func=AF.Exp, scale=-0.5), [ssg], [ssg])
            act_(lambda e: e.activation(out=G, in_=gtm[:, sub, :], func=AF.Silu), [gtm[:, sub, :]], [G])
            def f_yg(e):
                ins = None
                for hf in range(2):
                    ins = e.tensor_tensor(out=yg[:, hf * 512:(hf + 1) * 512].rearrange("p (h v) -> p h v", h=2),
                                          in0=ps[bo2[hf]][:, :].rearrange("p (h v) -> p h v", h=2),
                                          in1=bc3(ssg[:, hf * 2:hf * 2 + 2], 256), op=ALU.mult)
                return ins
            dve_(f_yg, [ps[bo2[0]][:, :], ps[bo2[1]][:, :], ssg], [yg])
            dve_(lambda e: e.tensor_tensor(out=yg, in0=yg, in1=G, op=ALU.mult), [yg, G], [yg])
            if l == 0 and ti == 0 and sub == 0:
                dbg("yg", yg)
            for half in range(2):
                b = nb()
                def f_tr2(e, half=half, b=b):
                    ins = None
                    for j in range(4):
                        c = half * 4 + j
                        ins = e.transpose(ps[b][:, j * 128:(j + 1) * 128], yg[:, c * 128:(c + 1) * 128], identf)
                    return ins
                pe_(f_tr2, [yg[:, half * 512:(half + 1) * 512], identf], [ps[b][:, :]])
                gc = base + O_GN
                dve_(lambda e, half=half, b=b, gc=gc: e.tensor_tensor(
                    out=hT[:, 8 + half * 4:8 + half * 4 + 4, c0:c0 + 128], in0=ps[b][:, :].rearrange("p (j n) -> p j n", j=4),
                    in1=bc3(pt[:, gc:gc + 4], 128), op=ALU.mult),
                    [ps[b][:, :], pt[:, gc:gc + 4]], [hT[:, 8 + half * 4:8 + half * 4 + 4, c0:c0 + 128]])
            R.off = mark

        def tap(name, ap):
            if l == 0 and ti == 0:
                dbg(name, ap)

        def job_resid(cb):
            def consumer(view):
                if cb == 0:
                    tap("xT0", xT[:])
                    tap("mixT", hT[:])
                    tap("xbcT", st_["xbcT"])
                    tap("qkT", st_["qkT"])
                    tap("gkT", st_["gkT"][0:17, :])
                    tap("ztm", st_["ztm"])
                    tap("vtm", st_["vtm"])
                    tap("gtm", st_["gtm"])
                    tap("ktm", st_["ktm"])
                    tap("dtraw", st_["dtraw"])
                    tap("ssdS", ssdS[l][:])
                    tap("glaS", glaS[l][:])
                for j in range(4):
                    b = nb()
                    fm_matmul(view, j * 128, 128, hT, b)
                    xc = xT[:, cb * 4 + j, :]
                    dve_(lambda e, xc=xc, b=b: e.tensor_tensor(out=xc, in0=xc, in1=ps[b][:, :], op=ALU.add),
                         [xc, ps[b][:, :]], [xc])
            return consumer

        def begin_ffn():
            R.off = 0
            rmsnorm_to_hT(base + O_NFFN)
            st_["actT"] = R.alloc([44, T], BF16)
            st_["sg"] = [R.alloc([T], F32) for _ in range(4)]
            st_["fcin"] = [R.alloc([516], F32) for _ in range(2)]
            st_["facc"] = [R.alloc([T], F32) for _ in range(2)]
            st_["fi"] = 0

        def ffn_conv(bank, ch):
            i = st_["fi"] % 2
            st_["fi"] += 1
            cin, acc = st_["fcin"][i], st_["facc"][i]
            act_(lambda e: e.copy(out=cin[:, 2:514], in_=ps[bank][:, :]), [ps[bank][:, :]], [cin[:, 2:514]])
            wc = base + O_FCW
            bcol = base + O_FCB + ch

            dve_(lambda e: e.tensor_copy(out=cin[:, 0:2], in_=fhalo[l][:, ch, :]), [fhalo[l][:, ch, :]], [cin[:, 0:2]])
            dve_(lambda e: e.tensor_scalar(out=acc, in0=cin[:, 0:512], scalar1=pt[:, wc + ch:wc + ch + 1],
                                           scalar2=pt[:, bcol:bcol + 1], op0=ALU.mult, op1=ALU.add),
                 [cin[:, 0:512]], [acc])
            for j in range(1, 3):
                dve_(lambda e, j=j: e.scalar_tensor_tensor(out=acc, in0=cin[:, j:j + 512],
                                                           scalar=pt[:, wc + 88 * j + ch:wc + 88 * j + ch + 1], in1=acc,
                                                           op0=ALU.mult, op1=ALU.add),
                     [cin[:, j:j + 512], acc], [acc])
            dve_(lambda e: e.tensor_copy(out=fhalo[l][:, ch, :], in_=cin[:, 512:514]), [cin[:, 512:514]], [fhalo[l][:, ch, :]])
            return acc

        def job_up_gate(bi):
            def consumer(view):
                if bi == 0:
                    tap("x1", xT[:])
                    begin_ffn()
                    tap("hT1", hT[:])
                for j in range(4):
                    b = nb()
                    fm_matmul(view, j * 128, 128, hT, b)
                    acc = ffn_conv(b, bi * 4 + j)
                    sg = st_["sg"][j]
                    act_(lambda e, acc=acc, sg=sg: e.activation(out=sg, in_=acc, func=AF.Silu), [acc], [sg])
            return consumer

        def job_up_val(bi):
            def consumer(view):
                for j in range(4):
                    b = nb()
                    fm_matmul(view, j * 128, 128, hT, b)
                    acc = ffn_conv(b, 44 + bi * 4 + j)
                    sg = st_["sg"][j]
                    dst = st_["actT"][:, bi * 4 + j, :]
                    dve_(lambda e, acc=acc, sg=sg, dst=dst: e.tensor_tensor(out=dst, in0=acc, in1=sg, op=ALU.mult),
                         [acc, sg], [dst])
            return consumer

        def job_down(cb, kq):
            def consumer(view):
                if kq == 0:
                    st_["dbanks"] = [nb() for _ in range(4)]
                banks = st_["dbanks"]
                actT = st_["actT"]
                for j in range(4):
                    def fn(e, j=j):
                        ins = None
                        for k in range(11):
                            ins = e.matmul(ps[banks[j]][:, :], lhsT=view[:, k, j * 128:(j + 1) * 128],
                                           rhs=actT[:, kq * 11 + k, :], start=(kq == 0 and k == 0),
                                           stop=(kq == 3 and k == 10))
                        return ins
                    pe_(fn, [view[:, :, j * 128:(j + 1) * 128], actT[:, kq * 11:kq * 11 + 11, :]], [ps[banks[j]][:, :]])
                if kq == 3:
                    for j in range(4):
                        b = banks[j]
                        xc = xT[:, cb * 4 + j, :]
                        dve_(lambda e, xc=xc, b=b: e.tensor_tensor(out=xc, in0=xc, in1=ps[b][:, :], op=ALU.add),
                             [xc, ps[b][:, :]], [xc])
            return consumer

        def job_ple_proj(view):
            tap("x2", xT[:])
            tap("actT", st_["actT"])
            R.off = 0
            rmsnorm_to_hT(base + O_NPLE)
            pst = R.alloc([NSUB, PLE], F32)
            pT = R.alloc([2, T], BF16)
            ppT = R.alloc([KC, T], F32)
            st_["ppT"] = ppT
            st_["sig"] = [R.alloc([T], F32) for _ in range(2)]
            ld(pst, p_d[l, t0:t0 + T, :].rearrange("(s p) c -> p s c", p=128))
            for c in range(2):
                b = nb()
                def f_tp(e, c=c, b=b):
                    ins = None
                    for sub in range(NSUB):
                        ins = e.transpose(ps[b][:, sub * 128:(sub + 1) * 128], pst[:, sub, c * 128:(c + 1) * 128], identf)
                    return ins
                pe_(f_tp, [pst, identf], [ps[b][:, :]])
                act_(lambda e, c=c, b=b: e.copy(out=pT[:, c, :], in_=ps[b][:, :]), [ps[b][:, :]], [pT[:, c, :]])
            for c in range(KC):
                b = nb()
                fm_matmul(view, c * 128, 128, pT, b, nk=2)
                act_(lambda e, c=c, b=b: e.copy(out=ppT[:, c, :], in_=ps[b][:, :]), [ps[b][:, :]], [ppT[:, c, :]])

        def job_ple_gate(cb):
            def consumer(view):
                ppT = st_["ppT"]
                for j in range(4):
                    b = nb()
                    fm_matmul(view, j * 128, 128, hT, b)
                    c = cb * 4 + j
                    sg = st_["sig"][c % 2]
                    act_(lambda e, sg=sg, b=b: e.activation(out=sg, in_=ps[b][:, :], func=AF.Sigmoid), [ps[b][:, :]], [sg])
                    xc = xT[:, c, :]
                    dve_(lambda e, sg=sg, c=c: e.tensor_tensor(out=sg, in0=sg, in1=ppT[:, c, :], op=ALU.mult),
                         [sg, ppT[:, c, :]], [sg])
                    dve_(lambda e, sg=sg, xc=xc: e.tensor_tensor(out=xc, in0=xc, in1=sg, op=ALU.add), [xc, sg], [xc])
            return consumer

        w_in = dr["w_in"][l]
        def wcols(wm, c0, n):
            return wm[:, c0:c0 + n]
        first = [True]
        def with_begin(cons):
            def consumer(view):
                begin_mixer()
                cons(view)
            return consumer
        add_job(wcols(w_in, 0, 512), KC, 512, with_begin(job_fm_conv(0)))
        add_job(wcols(w_in, 512, 512), KC, 512, job_fm_conv(4))
        add_job(wcols(w_in, 1024, 512), KC, 512, job_tm("ztm", 0, True))
        add_job(wcols(w_in, 1536, 512), KC, 512, job_tm("ztm", 512, False))
        add_job(wcols(w_in, 2048, 512), KC, 512, job_fm_conv(8))
        add_job(wcols(w_in, 2560, 528), KC, 528, job_dt_q)
        add_job(wcols(w_in, 3088, 512), KC, 512, job_k)
        add_job(wcols(w_in, 3600, 512), KC, 512, job_tm("vtm", 0, True))
        add_job(wcols(w_in, 4112, 512), KC, 512, job_tm("vtm", 512, False))
        add_job(wcols(w_in, 4624, 512), KC, 512, job_tm("gtm", 0, False))
        add_job(wcols(w_in, 5136, 528), KC, 528, job_g1_gk)
        for cb in range(4):
            add_job(wcols(dr["w_out"][l], cb * 512, 512), KC, 512, job_resid(cb))
        for bi in range(11):
            add_job(wcols(dr["ffn_w_up"][l], bi * 512, 512), KC, 512, job_up_gate(bi))
            add_job(wcols(dr["ffn_w_up"][l], DFF + bi * 512, 512), KC, 512, job_up_val(bi))
        for cb in range(4):
            for kq in range(4):
                add_job(dr["ffn_w_down"][l][kq * 1408:(kq + 1) * 1408, cb * 512:(cb + 1) * 512], 11, 512, job_down(cb, kq))
        add_job(dr["ple_w_proj"][l], 2, D, job_ple_proj)
        for cb in range(4):
            add_job(wcols(dr["ple_w_gate"][l], cb * 512, 512), KC, 512, job_ple_gate(cb))

    def load_x_tile(ti):
        t0 = ti * T
        R.off = 0
        xs = [R.alloc([D], F32) for _ in range(2)]
        for sub in range(NSUB):
            xsb = xs[sub % 2]
            ld(xsb, x_d[t0 + sub * 128:t0 + (sub + 1) * 128, :])
            for kg in range(4):
                b = nb()
                def f_t(e, kg=kg, b=b, xsb=xsb):
                    ins = None
                    for j in range(4):
                        k = kg * 4 + j
                        ins = e.transpose(ps[b][:, j * 128:(j + 1) * 128], xsb[:, k * 128:(k + 1) * 128], identf)
                    return ins
                pe_(f_t, [xsb[:, kg * 512:(kg + 1) * 512], identf], [ps[b][:, :]])
                dst = xT[:, kg * 4:kg * 4 + 4, sub * 128:(sub + 1) * 128]
                act_(lambda e, dst=dst, b=b: e.copy(out=dst, in_=ps[b][:, :].rearrange("p (j n) -> p j n", j=4)),
                     [ps[b][:, :]], [dst])

    def store_y_tile(ti):
        t0 = ti * T
        if ti == 0:
            dbg("x3", xT[:])
        R.off = 0
        yT = R.alloc([KC, T], F32)
        rmsnorm_to_hT(O_NFIN, out_f32=yT)
        ys = [R.alloc([D], F32) for _ in range(2)]
        for sub in range(NSUB):
            ysb = ys[sub % 2]
            for kg in range(4):
                b = nb()
                def f_t(e, kg=kg, b=b, sub=sub):
                    ins = None
                    for j in range(4):
                        k = kg * 4 + j
                        ins = e.transpose(ps[b][:, j * 128:(j + 1) * 128], yT[:, k, sub * 128:(sub + 1) * 128], identf)
                    return ins
                pe_(f_t, [yT[:, kg * 4:kg * 4 + 4, sub * 128:(sub + 1) * 128], identf], [ps[b][:, :]])
                dst = ysb[:, kg * 512:(kg + 1) * 512]
                if kg % 2 == 0:
                    act_(lambda e, dst=dst, b=b: e.copy(out=dst, in_=ps[b][:, :]), [ps[b][:, :]], [dst])
                else:
                    dve_(lambda e, dst=dst, b=b: e.tensor_copy(out=dst, in_=ps[b][:, :]), [ps[b][:, :]], [dst])
            st(y_d[t0 + sub * 128:t0 + (sub + 1) * 128, :], ysb)

    hooks_before = {}
    hooks_after = {}
    for ti in range(ntile):
        hooks_before[len(jobs)] = (lambda ti=ti: load_x_tile(ti))
        for l in range(depth):
            layer_tile(l, ti)
        hooks_after[len(jobs) - 1] = (lambda ti=ti: store_y_tile(ti))

    views = {}
    views[0] = issue_load(0)
    for ji in range(len(jobs)):
        if ji in hooks_before:
            hooks_before[ji]()
        if ji + 1 < len(jobs):
            views[ji + 1] = issue_load(ji + 1)
        jobs[ji][3](views[ji])
        del views[ji]
        if ji in hooks_after:
            hooks_after[ji]()

    S_.plan()
    sems = {n: es.enter_context(nc.semaphore(n.replace(":", "_"))) for n in S_.sem_names()}
    with nc.Block() as block:
        S_.emit(block, sems)
    es.close()
    return nc, sorted(dbg_out.keys()), len(S_.ops)


_CACHE = {}


def _run(inputs, debug=(), n_cores=None, trace=False):
    x = np.ascontiguousarray(np.asarray(inputs["x"], dtype=np.float32))
    p = np.asarray(inputs["p"], dtype=np.float32)
    B, S, _ = x.shape
    depth = p.shape[0]
    key = (S, depth, tuple(debug))
    if key not in _CACHE:
        _CACHE[key] = build_program(S, depth, debug)
    nc, dbg_names, _ = _CACHE[key]
    shared = {n: np.ascontiguousarray(np.asarray(inputs[n], dtype=np.float32)) for n in PARAM_NAMES}
    shared["norm_final"] = np.ascontiguousarray(np.asarray(inputs["norm_final"], dtype=np.float32))
    shared["cst"] = _consts_np()
    in_maps = []
    for b in range(B):
        m = dict(shared)
        m["x"] = np.ascontiguousarray(x[b])
        m["p"] = np.ascontiguousarray(p[:, b])
        in_maps.append(m)
    res = run_bass_kernel_spmd(nc, in_maps, core_ids=list(range(B)), **({"trace": True} if trace else {}))
    y = np.stack([np.asarray(r["y"]) for r in res.results], axis=0).astype(np.float32)
    if debug or trace:
        return y, res
    return y


def kernel(**inputs):
    return _run(inputs)
```
